# Optimizing a Trainium2 kernel written in Bass

```python
import jax, jax.numpy as jnp
from jax import lax
import numpy as np

D_MODEL = 2048
BATCH = 2
SEQ = 8192
DEPTH = 4

N_MEM = 256
NUM_MIXERS = 2
N_A = (DEPTH + 1) // 2
N_B = DEPTH // 2
MIX_WIDTH = (3 * D_MODEL) // 4
XA_HEADS = 4
XA_WIDTH = D_MODEL - MIX_WIDTH
XA_DIM = XA_WIDTH // XA_HEADS
HGRN_HEAD_DIM = 128
HGRN_HEADS = MIX_WIDTH // HGRN_HEAD_DIM
GLA_HEADS = 4
GLA_KEY_WIDTH = MIX_WIDTH // 2
GLA_DK = GLA_KEY_WIDTH // GLA_HEADS
GLA_DV = MIX_WIDTH // GLA_HEADS
GLA_GATE_RANK = 16
GLA_GATE_NORMALIZER = 16.0
CHUNK = 64
D_FF = ((8 * D_MODEL // 3 + 255) // 256) * 256
EPS = 1e-6
HGRN_IN = 4 * MIX_WIDTH + XA_WIDTH
GLA_IN = 2 * GLA_KEY_WIDTH + 2 * MIX_WIDTH + GLA_GATE_RANK + XA_WIDTH

kernel_name = "hgrn2_gla_interleaved_memxattn_trunk"


def _rmsnorm(x, gain):
    xf = x.astype(jnp.float32)
    y = xf * lax.rsqrt(jnp.mean(xf * xf, axis=-1, keepdims=True) + EPS)
    return (y * gain.astype(jnp.float32)).astype(x.dtype)


def _split_heads(t, n_heads):
    return t.reshape(t.shape[:-1] + (n_heads, t.shape[-1] // n_heads))


def _chunk_gated_linear_attention(q, k, v, log_a):
    b, t, h, dk = q.shape
    dv = v.shape[-1]
    n = t // CHUNK

    def to_chunks(u):
        return u.reshape(b, n, CHUNK, h, u.shape[-1]).transpose(1, 0, 3, 2, 4)

    mask = jnp.tril(jnp.ones((CHUNK, CHUNK), dtype=bool))

    def step(state, inp):
        qc, kc, vc, gc = inp
        cum = jnp.cumsum(gc, axis=2)
        inter = jnp.einsum('bhtd,bhde->bhte', qc * jnp.exp(cum), state)
        diff = cum[:, :, :, None, :] - cum[:, :, None, :, :]
        decay = jnp.exp(jnp.where(mask[:, :, None], diff, -jnp.inf))
        scores = jnp.einsum('bhtd,bhsd,bhtsd->bhts', qc, kc, decay)
        intra = jnp.einsum('bhts,bhse->bhte', scores, vc)
        last = cum[:, :, -1, :]
        new_state = jnp.exp(last)[..., None] * state + jnp.einsum(
            'bhsd,bhse->bhde', kc * jnp.exp(last[:, :, None, :] - cum), vc)
        return new_state, inter + intra

    s0 = jnp.zeros((b, h, dk, dv), jnp.float32)
    _, out = lax.scan(step, s0, (to_chunks(q), to_chunks(k), to_chunks(v), to_chunks(log_a)))
    return out.transpose(1, 0, 3, 2, 4).reshape(b, t, h, dv)


def _gated_head_norm(o, gain, gate_raw, n_heads):
    of = o * lax.rsqrt(jnp.mean(o * o, axis=-1, keepdims=True) + EPS) * gain.astype(jnp.float32)
    gate = jax.nn.silu(_split_heads(gate_raw.astype(jnp.float32), n_heads))
    y = of * gate
    return y.reshape(y.shape[:2] + (n_heads * y.shape[-1],))


def _hgrn2_mixer(hn, w_in, lb, onorm):
    f32 = jnp.float32
    proj = hn @ w_in
    q, f, i, g, xq = jnp.split(proj, [MIX_WIDTH, 2 * MIX_WIDTH, 3 * MIX_WIDTH, 4 * MIX_WIDTH], axis=-1)
    f = lb + (1.0 - lb) * jax.nn.sigmoid(f.astype(f32))
    log_f = jnp.log(f)
    k = 1.0 - f
    q = jax.nn.silu(q.astype(f32))
    o = _chunk_gated_linear_attention(
        _split_heads(q, HGRN_HEADS), _split_heads(k, HGRN_HEADS),
        _split_heads(i.astype(f32), HGRN_HEADS), _split_heads(log_f, HGRN_HEADS))
    y = _gated_head_norm(o, onorm, g, HGRN_HEADS)
    return y.astype(hn.dtype), xq


def _gla_mixer(hn, w_in, w_gk, b_gk, onorm):
    f32 = jnp.float32
    proj = hn @ w_in
    kw, w, r = GLA_KEY_WIDTH, MIX_WIDTH, GLA_GATE_RANK
    q, k, v, g, gk_low, xq = jnp.split(
        proj, [kw, 2 * kw, 2 * kw + w, 2 * kw + 2 * w, 2 * kw + 2 * w + r], axis=-1)
    log_a = jax.nn.log_sigmoid((gk_low @ w_gk + b_gk).astype(f32)) / GLA_GATE_NORMALIZER
    q = q.astype(f32) * (GLA_DK ** -0.5)
    o = _chunk_gated_linear_attention(
        _split_heads(q, GLA_HEADS), _split_heads(k.astype(f32), GLA_HEADS),
        _split_heads(v.astype(f32), GLA_HEADS), _split_heads(log_a, GLA_HEADS))
    y = _gated_head_norm(o, onorm, g, GLA_HEADS)
    return y.astype(hn.dtype), xq


def _memory_attention(xq, mem_n, w_kv):
    f32 = jnp.float32
    k, v = jnp.split(mem_n @ w_kv, 2, axis=-1)
    q = _split_heads(xq, XA_HEADS).astype(f32)
    k = _split_heads(k, XA_HEADS).astype(f32)
    v = _split_heads(v, XA_HEADS).astype(f32)
    s = jnp.einsum('bthd,bmhd->bhtm', q, k) * (XA_DIM ** -0.5)
    p = jax.nn.softmax(s, axis=-1)
    o = jnp.einsum('bhtm,bmhd->bthd', p, v)
    return o.reshape(o.shape[:2] + (XA_WIDTH,)).astype(xq.dtype)


def setup_inputs(seed: int = 0) -> dict:
    key = jax.random.key(seed)
    ks = jax.random.split(key, 20)
    f32 = jnp.float32

    def nrm(k, shape, scale):
        return jax.random.normal(k, shape, f32) * scale

    def gain(k, shape):
        return 1.0 + 0.02 * jax.random.normal(k, shape, f32)

    return {
        "x": jax.random.normal(ks[0], (BATCH, SEQ, D_MODEL), f32),
        "mem": jax.random.normal(ks[1], (BATCH, N_MEM, D_MODEL), f32),
        "norm_mix": gain(ks[2], (DEPTH, D_MODEL)),
        "norm_ffn": gain(ks[3], (DEPTH, D_MODEL)),
        "norm_mem": gain(ks[4], (D_MODEL,)),
        "norm_final": gain(ks[5], (D_MODEL,)),
        "hgrn_w_in": nrm(ks[6], (N_A, D_MODEL, HGRN_IN), D_MODEL ** -0.5),
        "hgrn_lb_logits": nrm(ks[7], (N_A, MIX_WIDTH), 0.5),
        "hgrn_onorm": gain(ks[8], (N_A, HGRN_HEAD_DIM)),
        "gla_w_in": nrm(ks[9], (N_B, D_MODEL, GLA_IN), D_MODEL ** -0.5),
        "gla_w_gk": nrm(ks[10], (N_B, GLA_GATE_RANK, GLA_KEY_WIDTH), GLA_GATE_RANK ** -0.5),
        "gla_b_gk": nrm(ks[11], (N_B, GLA_KEY_WIDTH), 0.01),
        "gla_onorm": gain(ks[12], (N_B, GLA_DV)),
        "w_mem_kv": nrm(ks[13], (DEPTH, D_MODEL, 2 * XA_WIDTH), D_MODEL ** -0.5),
        "w_out": nrm(ks[14], (DEPTH, D_MODEL, D_MODEL), D_MODEL ** -0.5),
        "w_gate_up": nrm(ks[15], (DEPTH, D_MODEL, 2 * D_FF), D_MODEL ** -0.5),
        "w_down": nrm(ks[16], (DEPTH, D_FF, D_MODEL), D_FF ** -0.5),
    }


def reference(x, mem, norm_mix, norm_ffn, norm_mem, norm_final,
              hgrn_w_in, hgrn_lb_logits, hgrn_onorm,
              gla_w_in, gla_w_gk, gla_b_gk, gla_onorm,
              w_mem_kv, w_out, w_gate_up, w_down):
    mem_n = _rmsnorm(mem, norm_mem)
    cs = jnp.cumsum(jax.nn.softmax(hgrn_lb_logits.astype(jnp.float32), axis=0), axis=0)
    lbs = cs - cs[0:1]

    h = x
    for layer in range(DEPTH):
        hn = _rmsnorm(h, norm_mix[layer])
        j = layer // NUM_MIXERS
        if layer % NUM_MIXERS == 0:
            y, xq = _hgrn2_mixer(hn, hgrn_w_in[j], lbs[j], hgrn_onorm[j])
        else:
            y, xq = _gla_mixer(hn, gla_w_in[j], gla_w_gk[j], gla_b_gk[j], gla_onorm[j])
        xa = _memory_attention(xq, mem_n, w_mem_kv[layer])
        h = h + jnp.concatenate([y, xa], axis=-1) @ w_out[layer]

        hn = _rmsnorm(h, norm_ffn[layer])
        gate, up = jnp.split(hn @ w_gate_up[layer], 2, axis=-1)
        h = h + (jax.nn.silu(gate) * up) @ w_down[layer]
    return _rmsnorm(h, norm_final)
```

```python
import numpy as np
import concourse.bass as bass
import concourse.mybir as mybir
from concourse.bass_utils import run_bass_kernel_spmd

F32 = mybir.dt.float32
BF16 = mybir.dt.bfloat16
AF = mybir.ActivationFunctionType
ALU = mybir.AluOpType
AX = mybir.AxisListType

D = 2048
KC = 16
T = 1024
NT = 2
NTL = 8
DEPTH = 4
NMEM = 256
MIXW = 1536
DFF = 5632
NFF = 44
HGRN_IN = 6656
GLA_IN = 5136
EPS = 1e-6
NCORES = 8
RING = 8


class Res:
    __slots__ = ("lw", "rd")

    def __init__(self):
        self.lw = None
        self.rd = {}


class Prog:
    ENG = ("pe", "act", "dve", "pool", "sp")
    EPOCH = 30000

    def __init__(self, nc, n_dma_sems=40):
        self.nc = nc
        self.rec = {e: [] for e in self.ENG}
        self.cnt = {e: 0 for e in self.ENG}
        self.sem = {e: nc.alloc_semaphore("c_" + e) for e in self.ENG}
        self.nsem = {e: 0 for e in self.ENG}
        self.seen = {e: {} for e in self.ENG}
        self.dsem = [nc.alloc_semaphore("d%d" % i) for i in range(n_dma_sems)]
        self.dcnt = [0] * n_dma_sems
        self.dnext = 0
        self.nops = 0

    def res(self):
        return Res()

    def _wait(self, eng, ev):
        sem, val, src = ev
        key = id(sem)
        if self.seen[eng].get(key, 0) >= val:
            return
        self.seen[eng][key] = val
        self.rec[eng].append(lambda e, s=sem, v=val: e.wait_ge(s, v))

    def _deps(self, eng, reads, writes):
        for r in reads:
            if r.lw is not None:
                self._dep1(eng, r.lw)
        for w in writes:
            if w.lw is not None:
                self._dep1(eng, w.lw)
            for ev in w.rd.values():
                self._dep1(eng, ev)

    def _dep1(self, eng, ev):
        if ev[2] == eng and eng == "pe":
            return
        self._wait(eng, ev)

    def _commit(self, ev, reads, writes):
        key = id(ev[0])
        for r in reads:
            old = r.rd.get(key)
            if old is None or old[1] < ev[1]:
                r.rd[key] = ev
        for w in writes:
            w.lw = ev
            w.rd = {}

    def op(self, eng, fn, reads=(), writes=()):
        self._deps(eng, reads, writes)
        if self.cnt[eng] >= self.EPOCH:
            self.nsem[eng] += 1
            self.sem[eng] = self.nc.alloc_semaphore("c_%s_%d" % (eng, self.nsem[eng]))
            self.cnt[eng] = 0
        self.cnt[eng] += 1
        self.nops += 1
        sem = self.sem[eng]
        self.rec[eng].append(lambda e, f=fn, s=sem: f(e).then_inc(s, 1))
        ev = (sem, self.cnt[eng], eng)
        self._commit(ev, reads, writes)
        return ev

    def dma(self, q, out, in_, reads=(), writes=()):
        self._deps(q, reads, writes)
        i = self.dnext
        self.dnext = (self.dnext + 1) % len(self.dsem)
        sem = self.dsem[i]
        if self.dcnt[i] > 0:
            self._wait(q, (sem, self.dcnt[i], "dma"))
        self.dcnt[i] += 16
        self.nops += 1
        self.rec[q].append(lambda e, o=out, a=in_, s=sem: e.dma_start(out=o, in_=a).then_inc(s, 16))
        ev = (sem, self.dcnt[i], "dma")
        self._commit(ev, reads, writes)
        return ev

    def barrier(self):
        for e in self.ENG:
            for e2 in self.ENG:
                if e2 != e and self.cnt[e2] > 0:
                    self._wait(e, (self.sem[e2], self.cnt[e2], e2))

    def finish(self):
        for i, s in enumerate(self.dsem):
            if self.dcnt[i]:
                self._wait("sp", (s, self.dcnt[i], "dma"))
        for e in self.ENG:
            if e != "sp" and self.cnt[e] > 0:
                self._wait("sp", (self.sem[e], self.cnt[e], e))

    def emit(self):
        with self.nc.Block() as block:
            @block.tensor
            def _(e):
                for f in self.rec["pe"]:
                    f(e)

            @block.scalar
            def _(e):
                for f in self.rec["act"]:
                    f(e)

            @block.vector
            def _(e):
                for f in self.rec["dve"]:
                    f(e)

            @block.gpsimd
            def _(e):
                for f in self.rec["pool"]:
                    f(e)

            @block.sync
            def _(e):
                for f in self.rec["sp"]:
                    f(e)


def hgrn_heads():
    return [dict(blocks=[(h, 0, 128)], dv=128, nj=1, ych=h) for h in range(12)]


def gla_heads():
    blk = {0: [(0, 0, 128), (1, 0, 64)], 1: [(1, 64, 128), (2, 0, 128)],
           2: [(3, 0, 128), (4, 0, 64)], 3: [(4, 64, 128), (5, 0, 128)]}
    return [dict(blocks=blk[h], dv=384, nj=3, ych=3 * h) for h in range(4)]


class Builder:
    PAR = {"nmix": 0, "nffn": 64, "nmem": 128, "nfin": 144, "lbl": 160, "onh": 184, "ong": 186, "bgk": 192}
    NPAR = 204

    def __init__(self, steps):
        self.steps = steps
        nc = bass.Bass("TRN2", target_bir_lowering=False)
        self.nc = nc
        self.P = Prog(nc)
        self.layers = sorted({l for st in steps for l in st["layers"]})
        self._declare_io()
        self._alloc()

    def _declare_io(self):
        nc = self.nc
        ns = len(self.steps)
        self.any_x = any(not st["in_fm"] for st in self.steps)
        self.any_fm_in = any(st["in_fm"] for st in self.steps)
        self.any_final = any(st["final"] for st in self.steps)
        self.any_fm_out = any(not st["final"] for st in self.steps)
        if self.any_x:
            self.hin = nc.dram_tensor("hin", [ns, T, D], F32, kind="ExternalInput").ap()
        if self.any_fm_in:
            self.hfm_in = nc.dram_tensor("hfm_in", [ns, 128, KC * T], F32, kind="ExternalInput").ap()
        if self.any_final:
            self.hout = nc.dram_tensor("hout", [ns, T, D], F32, kind="ExternalOutput").ap()
        if self.any_fm_out:
            self.hfm_out = nc.dram_tensor("hfm_out", [ns, 128, KC * T], F32, kind="ExternalOutput").ap()
        self.mem = nc.dram_tensor("mem", [ns, NMEM, D], F32, kind="ExternalInput").ap()
        self.consts = nc.dram_tensor("consts", [128, 384], F32, kind="ExternalInput").ap()
        self.cmask = nc.dram_tensor("cmask", [128, 16], F32, kind="ExternalInput").ap()
        self.params = nc.dram_tensor("params", [128, self.NPAR], F32, kind="ExternalInput").ap()
        self.wgk = nc.dram_tensor("wgk", [2, 16, 768], F32, kind="ExternalInput").ap()
        self.w_in, self.w_kv, self.w_out, self.w_gu, self.w_dn = {}, {}, {}, {}, {}
        for l in self.layers:
            win = HGRN_IN if l % 2 == 0 else GLA_IN
            self.w_in[l] = nc.dram_tensor("w_in%d" % l, [D, win], F32, kind="ExternalInput").ap()
            self.w_kv[l] = nc.dram_tensor("w_kv%d" % l, [D, 1024], F32, kind="ExternalInput").ap()
            self.w_out[l] = nc.dram_tensor("w_out%d" % l, [D, D], F32, kind="ExternalInput").ap()
            self.w_gu[l] = nc.dram_tensor("w_gu%d" % l, [D, 2 * DFF], F32, kind="ExternalInput").ap()
            self.w_dn[l] = nc.dram_tensor("w_dn%d" % l, [DFF, D], F32, kind="ExternalInput").ap()

    def _alloc(self):
        nc, P = self.nc, self.P
        A = nc.alloc_sbuf_tensor
        self.h = A("h", [128, KC, T], F32)
        self.r_h = [[P.res() for _ in range(NT)] for _ in range(KC)]
        self.hn = A("hn", [128, KC, T], BF16)
        self.r_hn = P.res()
        self.NY = 10
        self.yT = A("yT", [128, self.NY, T], BF16)
        self.r_y = [P.res() for _ in range(self.NY)]
        self.ring = [A("ring%d" % i, [128, KC, 128], BF16) for i in range(RING)]
        self.r_ring = [P.res() for _ in range(RING)]
        from collections import deque
        self.ring_q = deque(range(RING))
        NTMP = 6
        self.tmp = [A("tmp%d" % i, [128, 512], F32) for i in range(NTMP)]
        self.r_tmp = [P.res() for _ in range(NTMP)]
        self.tmp_i = 0
        self.rstd = A("rstd", [128, 512], F32)
        self.r_rstd = P.res()
        self.kg = [A("kg%d" % i, [128, T], BF16) for i in range(2)]
        self.qg = [A("qg%d" % i, [128, T], BF16) for i in range(2)]
        self.kdtok = [A("kdtok%d" % i, [128, NTL, 128], BF16) for i in range(2)]
        self.r_kg = [P.res() for _ in range(2)]
        self.r_qg = [P.res() for _ in range(2)]
        self.r_kdtok = [P.res() for _ in range(2)]
        self.kdT = A("kdT", [128, T], BF16)
        self.r_kdT = P.res()
        self.vtok = A("vtok", [128, NTL, 384], BF16)
        self.r_vtok = P.res()
        self.gate = A("gate", [128, 3, T], BF16)
        self.r_gate = P.res()
        self.Acol = [A("Acol%d" % i, [128, 16], F32) for i in range(2)]
        self.r_Acol = [P.res() for _ in range(2)]
        self.S = [A("S%d" % i, [128, 385], F32) for i in range(2)]
        self.r_S = [P.res() for _ in range(2)]
        self.Sbf = [[A("Sbf%d_%d" % (i, j), [128, 384], BF16) for j in range(3)] for i in range(2)]
        self.r_Sbf = [[P.res() for j in range(3)] for i in range(2)]
        self.sbf_i = [0, 0]
        self.Pm = [A("Pm%d" % i, [128, 128], BF16) for i in range(2)]
        self.r_Pm = [P.res() for _ in range(2)]
        self.pm_i = 0
        self.small = A("small", [128, 64], F32)
        self.r_small = P.res()
        self.cst = A("cst", [128, 128], F32)
        self.r_cst = P.res()
        self.cm = A("cm", [128, 16], F32)
        self.par = A("par", [128, self.NPAR], F32)
        self.r_par = P.res()
        self.maskb = A("maskb", [128, 128], BF16)
        self.identb = A("identb", [128, 128], BF16)
        self.onesb = A("onesb", [128, 128], BF16)
        self.smask = A("smask", [128, 512], BF16)
        self.wgk_sb = A("wgk_sb", [16, 768], BF16)
        self.nbgk = A("nbgk", [128, 12], F32)
        self.lbt = A("lbt", [128, 4, 12], F32)
        self.gklow = A("gklow", [16, T], BF16)
        self.r_gklow = P.res()
        self.bank = [nc.alloc_psum_tensor("bank%d" % i, [128, 512], F32) for i in range(8)]
        self.r_bank = [P.res() for _ in range(8)]
        self.acc_set = [0, 1]
        self.acc_i = 0
        self.sm_set = [2, 3]
        self.sm_i = 0
        print("sbuf bytes remaining", nc.sbuf_bytes_remaining, flush=True)

    def pcol(self, name, l=0, k=0):
        off = self.PAR[name]
        if name in ("nmix", "nffn"):
            c = off + l * 16 + k
        elif name in ("nmem", "nfin"):
            c = off + k
        elif name == "lbl":
            c = off + l * 12 + k
        elif name == "onh":
            c = off + l
        elif name == "ong":
            c = off + l * 3 + k
        elif name == "bgk":
            c = off + l * 6 + k
        return self.par[:, c:c + 1]

    def next_acc(self):
        b = self.acc_set[self.acc_i % len(self.acc_set)]
        self.acc_i += 1
        return b

    def next_sm(self):
        b = self.sm_set[self.sm_i % len(self.sm_set)]
        self.sm_i += 1
        return b

    def next_tmp(self):
        i = self.tmp_i % len(self.tmp)
        self.tmp_i += 1
        return i

    def pin(self):
        return self.ring_q.popleft()

    def unpin(self, s):
        self.ring_q.append(s)

    def load_w(self, src_ap, nk=KC, ncols=128):
        i = self.ring_q.popleft()
        self.ring_q.append(i)
        dst = self.ring[i][:, 0:nk, 0:ncols]
        self.P.dma("pool", dst, src_ap.rearrange("(k p) c -> p k c", p=128), writes=[self.r_ring[i]])
        return i

    def load_w_rows(self, src_ap):
        i = self.ring_q.popleft()
        self.ring_q.append(i)
        dst = self.ring[i][:].rearrange("p k c -> p (k c)")
        self.P.dma("pool", dst, src_ap, writes=[self.r_ring[i]])
        return i

    def gemm(self, slot, M, rhs_fn, nk, tt, reads, col0=0):
        b = self.next_acc()
        ps = self.bank[b]
        ring = self.ring[slot]

        def fn(e):
            ins = None
            for k in range(nk):
                ins = e.matmul(ps[0:M, :], ring[:, k, col0:col0 + M], rhs_fn(k, tt), start=(k == 0), stop=(k == nk - 1))
            return ins
        self.P.op("pe", fn, reads=[self.r_ring[slot]] + list(reads), writes=[self.r_bank[b]])
        return b

    def hn_rhs(self, k, tt):
        return self.hn[:, k, tt * 512:(tt + 1) * 512]

    def hs(self, k, tt):
        return self.h[:, k, tt * 512:(tt + 1) * 512]

    def prologue(self):
        P = self.P
        P.dma("sp", self.cst[:], self.consts[:, 256:384], writes=[self.r_cst])
        P.dma("pool", self.maskb[:], self.consts[:, 0:128], writes=[self.r_cst])
        P.dma("pool", self.identb[:], self.consts[:, 128:256], writes=[self.r_gklow])
        P.dma("sp", self.cm[:], self.cmask, writes=[self.r_par])
        P.dma("sp", self.par[:], self.params, writes=[self.r_par])
        P.barrier_lite = None
        c = self.r_cst
        P.op("dve", lambda e: e.memset(self.onesb[:], 1.0), reads=[self.r_gklow], writes=[c])
        P.op("dve", lambda e: e.memset(self.smask[:], 1.0), writes=[c])
        P.op("dve", lambda e: e.memset(self.smask[:].rearrange("p (c t) -> p c t", t=64)[:, :, 0:1], 0.0), reads=[c], writes=[c])
        P.op("dve", lambda e: e.tensor_scalar(self.nbgk[:], self.par[:, 192:204], -1.0, None, ALU.mult), reads=[self.r_par], writes=[c])
        sm = self.small
        rs = self.r_small
        l0 = self.par[:, 160:172]
        l1 = self.par[:, 172:184]
        P.op("act", lambda e: e.activation(sm[:, 0:12], l0, AF.Exp), reads=[self.r_par], writes=[rs])
        P.op("act", lambda e: e.activation(sm[:, 12:24], l1, AF.Exp), reads=[self.r_par, rs], writes=[rs])
        P.op("dve", lambda e: e.tensor_add(sm[:, 24:36], sm[:, 0:12], sm[:, 12:24]), reads=[rs], writes=[rs])
        P.op("dve", lambda e: e.reciprocal(sm[:, 24:36], sm[:, 24:36]), reads=[rs], writes=[rs])
        P.op("dve", lambda e: e.tensor_mul(sm[:, 0:12], sm[:, 0:12], sm[:, 24:36]), reads=[rs], writes=[rs])
        P.op("dve", lambda e: e.tensor_mul(sm[:, 12:24], sm[:, 12:24], sm[:, 24:36]), reads=[rs], writes=[rs])
        P.op("dve", lambda e: e.tensor_add(sm[:, 36:48], sm[:, 0:12], sm[:, 12:24]), reads=[rs], writes=[rs])
        P.op("dve", lambda e: e.tensor_sub(self.lbt[:, 0, :], sm[:, 0:12], sm[:, 0:12]), reads=[rs], writes=[c])
        P.op("dve", lambda e: e.tensor_sub(self.lbt[:, 2, :], sm[:, 36:48], sm[:, 0:12]), reads=[rs, c], writes=[c])
        for j in (0, 2):
            P.op("dve", lambda e, j=j: e.tensor_scalar(self.lbt[:, j + 1, :], self.lbt[:, j, :], -1.0, 1.0, ALU.mult, ALU.add),
                 reads=[c], writes=[c])

    def load_x(self, si):
        P = self.P
        identf = self.cst[:, 0:128]
        for j in range(NTL):
            tt = j // 4
            for kg in range(4):
                ti = self.next_tmp()
                tmp = self.tmp[ti]
                P.dma("sp", tmp[:], self.hin[si, j * 128:(j + 1) * 128, kg * 512:(kg + 1) * 512], writes=[self.r_tmp[ti]])
                b = self.next_acc()
                ps = self.bank[b]

                def fn(e, tmp=tmp, ps=ps):
                    ins = None
                    for kk in range(4):
                        ins = e.transpose(ps[:, kk * 128:(kk + 1) * 128], tmp[:, kk * 128:(kk + 1) * 128], identf)
                    return ins
                P.op("pe", fn, reads=[self.r_tmp[ti], self.r_cst], writes=[self.r_bank[b]])
                wr = [self.r_h[kg * 4 + kk][tt] for kk in range(4)]
                P.op("act", lambda e, ps=ps, kg=kg, j=j: e.activation(
                    self.h[:, kg * 4:(kg + 1) * 4, j * 128:(j + 1) * 128],
                    ps[:, :].rearrange("p (a b) -> p a b", b=128), AF.Copy),
                    reads=[self.r_bank[b]], writes=wr)

    def load_fm(self, si):
        P = self.P
        for k in range(KC):
            P.dma("sp", self.h[:, k, :], self.hfm_in[si, :, k * T:(k + 1) * T], writes=[self.r_h[k][0], self.r_h[k][1]])

    def store_fm(self, si):
        P = self.P
        for k in range(KC):
            P.dma("sp", self.hfm_out[si, :, k * T:(k + 1) * T], self.h[:, k, :], reads=[self.r_h[k][0], self.r_h[k][1]])

    def rms_stats(self, tt):
        P = self.P
        bss = self.next_acc()
        pss = self.bank[bss]
        for k in range(KC):
            ti = self.next_tmp()
            sq = self.tmp[ti][:].bitcast(BF16)[:, 0:512]
            P.op("act", lambda e, sq=sq, k=k: e.activation(sq, self.hs(k, tt), AF.Square),
                 reads=[self.r_h[k][tt]], writes=[self.r_tmp[ti]])
            P.op("pe", lambda e, sq=sq, k=k: e.matmul(pss[:, :], self.onesb[:], sq, start=(k == 0), stop=(k == KC - 1)),
                 reads=[self.r_tmp[ti], self.r_cst], writes=[self.r_bank[bss]])
        P.op("act", lambda e: e.activation(self.rstd[:], pss[:, :], AF.Sqrt, scale=1.0 / D, bias=EPS),
             reads=[self.r_bank[bss]], writes=[self.r_rstd])
        P.op("dve", lambda e: e.reciprocal(self.rstd[:], self.rstd[:]), reads=[self.r_rstd], writes=[self.r_rstd])

    def norm_to_hn(self, gname, l):
        P = self.P
        for tt in range(NT):
            self.rms_stats(tt)
            for k in range(KC):
                P.op("dve", lambda e, k=k, tt=tt: e.scalar_tensor_tensor(
                    self.hn_rhs(k, tt), self.hs(k, tt), self.pcol(gname, l, k), self.rstd[:], ALU.mult, ALU.mult),
                    reads=[self.r_h[k][tt], self.r_rstd, self.r_par], writes=[self.r_hn])

    def final_out(self, si):
        P = self.P
        identf = self.cst[:, 0:128]
        stage = self.hn[:].rearrange("p k t -> p (k t)").bitcast(F32)
        st_i = 0
        for tt in range(NT):
            self.rms_stats(tt)
            for kg in range(4):
                tis = []
                for kk in range(4):
                    k = kg * 4 + kk
                    ti = self.next_tmp()
                    tis.append(ti)
                    P.op("dve", lambda e, k=k, ti=ti, tt=tt: e.scalar_tensor_tensor(
                        self.tmp[ti][:], self.hs(k, tt), self.pcol("nfin", 0, k), self.rstd[:], ALU.mult, ALU.mult),
                        reads=[self.r_h[k][tt], self.r_rstd, self.r_par], writes=[self.r_tmp[ti]])
                for j4 in range(4):
                    b = self.next_acc()
                    ps = self.bank[b]

                    def fn(e, ps=ps, tis=tuple(tis), j4=j4):
                        ins = None
                        for kk in range(4):
                            ins = e.transpose(ps[:, kk * 128:(kk + 1) * 128], self.tmp[tis[kk]][:, j4 * 128:(j4 + 1) * 128], identf)
                        return ins
                    P.op("pe", fn, reads=[self.r_tmp[t] for t in tis] + [self.r_cst], writes=[self.r_bank[b]])
                    sl = st_i % 16
                    st_i += 1
                    sv = stage[:, sl * 512:(sl + 1) * 512]
                    P.op("act", lambda e, sv=sv, ps=ps: e.activation(sv, ps[:, :], AF.Copy),
                         reads=[self.r_bank[b]], writes=[self.r_hn])
                    j = tt * 4 + j4
                    P.dma("sp", self.hout[si, j * 128:(j + 1) * 128, kg * 512:(kg + 1) * 512], sv, reads=[self.r_hn])

    def xattn(self, si, l):
        P = self.P
        sm = self.small
        sm0, sm1, skv = self.pin(), self.pin(), self.pin()
        memn = [self.ring[sm0], self.ring[sm1]]
        r_memn = [self.r_ring[sm0], self.r_ring[sm1]]
        KT = self.ring[skv][:, 0:8, :].rearrange("p a b -> p (a b)").rearrange("p (h m) -> p h m", m=NMEM)
        Vm = self.ring[skv][:, 8:16, :].rearrange("p a b -> p (a b)").rearrange("p (h m) -> p h m", m=512)
        r_kv = self.r_ring[skv]
        for mh in range(2):
            st = self.yT[:, 4:8, :].rearrange("p a t -> p (a t)").bitcast(F32)
            rs = [self.r_y[i] for i in range(4, 8)]
            junk = self.yT[:, 8:10, :].rearrange("p a t -> p (a t)")
            rj = [self.r_y[8], self.r_y[9]]
            P.dma("sp", st, self.mem[si, mh * 128:(mh + 1) * 128, :], writes=rs)
            P.op("act", lambda e, st=st, junk=junk: e.activation(junk, st, AF.Square, accum_out=sm[:, 48:49]),
                 reads=rs, writes=rj + [self.r_small])
            P.op("act", lambda e: e.activation(sm[:, 49:50], sm[:, 48:49], AF.Sqrt, scale=1.0 / D, bias=EPS),
                 reads=[self.r_small], writes=[self.r_small])
            P.op("dve", lambda e: e.reciprocal(sm[:, 49:50], sm[:, 49:50]), reads=[self.r_small], writes=[self.r_small])
            P.op("dve", lambda e, st=st, junk=junk: e.tensor_scalar(junk, st, sm[:, 49:50], None, ALU.mult),
                 reads=rs + [self.r_small], writes=rj)
            for kg in range(4):
                b = self.next_sm()
                psb = self.bank[b].bitcast(BF16)

                def fn(e, psb=psb, kg=kg, junk=junk):
                    ins = None
                    for kk in range(4):
                        k = kg * 4 + kk
                        ins = e.transpose(psb[:, kk * 128:(kk + 1) * 128], junk[:, k * 128:(k + 1) * 128], self.identb[:])
                    return ins
                P.op("pe", fn, reads=rj + [self.r_cst], writes=[self.r_bank[b]])
                for kk in range(4):
                    k = kg * 4 + kk
                    P.op("dve", lambda e, psb=psb, kk=kk, k=k, mh=mh: e.tensor_scalar(
                        memn[mh][:, k, :], psb[:, kk * 128:(kk + 1) * 128], self.pcol("nmem", 0, k), None, ALU.mult),
                        reads=[self.r_bank[b], self.r_par], writes=[r_memn[mh]])
        wkv = self.w_kv[l]
        for hd in range(4):
            s = self.load_w(wkv[:, hd * 128:(hd + 1) * 128])
            b = self.next_acc()
            ps = self.bank[b]

            def fn(e, s=s, ps=ps):
                ins = None
                for mh in range(2):
                    for k in range(KC):
                        ins = e.matmul(ps[:, mh * 128:(mh + 1) * 128], self.ring[s][:, k, :], memn[mh][:, k, :],
                                       start=(k == 0), stop=(k == KC - 1))
                return ins
            P.op("pe", fn, reads=[self.r_ring[s]] + r_memn, writes=[self.r_bank[b]])
            P.op("act", lambda e, ps=ps, hd=hd: e.activation(KT[:, hd, :], ps[:, 0:NMEM], AF.Copy),
                 reads=[self.r_bank[b]], writes=[r_kv])
        for hd in range(4):
            s = self.load_w(wkv[:, 512 + hd * 128:512 + (hd + 1) * 128])
            for mh in range(2):
                b = self.next_acc()
                ps = self.bank[b]

                def fn(e, s=s, ps=ps, mh=mh):
                    ins = None
                    for k in range(KC):
                        ins = e.matmul(ps[:, 0:128], memn[mh][:, k, :], self.ring[s][:, k, :], start=(k == 0), stop=(k == KC - 1))
                    return ins
                P.op("pe", fn, reads=[self.r_ring[s]] + r_memn, writes=[self.r_bank[b]])
                P.op("act", lambda e, ps=ps, hd=hd, mh=mh: e.activation(Vm[:, mh, hd * 128:(hd + 1) * 128], ps[:, 0:128], AF.Copy),
                     reads=[self.r_bank[b]], writes=[r_kv])
        self.unpin(sm0)
        self.unpin(sm1)
        win = self.w_in[l]
        xq0 = (HGRN_IN if l % 2 == 0 else GLA_IN) - 512
        xq = self.kdT
        for hd in range(4):
            s = self.load_w(win[:, xq0 + hd * 128: xq0 + (hd + 1) * 128])
            for tt in range(NT):
                b = self.gemm(s, 128, self.hn_rhs, KC, tt, [self.r_hn])
                P.op("act", lambda e, b=b, tt=tt: e.activation(xq[:, tt * 512:(tt + 1) * 512], self.bank[b][:, :], AF.Copy),
                     reads=[self.r_bank[b]], writes=[self.r_kdT])
            for tt in range(NT):
                eti = [self.next_tmp(), self.next_tmp()]
                E = [self.tmp[t][:].bitcast(BF16)[:, 0:512] for t in eti]
                r_E = [self.r_tmp[t] for t in eti]
                for mh in range(2):
                    b = self.next_acc()
                    ps = self.bank[b]
                    P.op("pe", lambda e, ps=ps, mh=mh, tt=tt, hd=hd: e.matmul(
                        ps[:, :], KT[:, hd, mh * 128:(mh + 1) * 128], xq[:, tt * 512:(tt + 1) * 512], start=True, stop=True),
                        reads=[r_kv, self.r_kdT], writes=[self.r_bank[b]])
                    P.op("act", lambda e, ps=ps, Em=E[mh]: e.activation(Em, ps[:, :], AF.Exp, scale=128 ** -0.5),
                         reads=[self.r_bank[b]], writes=[r_E[mh]])
                bo = self.next_acc()
                pso = self.bank[bo]

                def fn(e, pso=pso, hd=hd, E=E):
                    e.matmul(pso[:, :], Vm[:, 0, hd * 128:(hd + 1) * 128], E[0], start=True, stop=False)
                    return e.matmul(pso[:, :], Vm[:, 1, hd * 128:(hd + 1) * 128], E[1], start=False, stop=True)
                P.op("pe", fn, reads=[r_kv] + r_E, writes=[self.r_bank[bo]])
                bd = self.next_sm()
                psd = self.bank[bd]

                def fn2(e, psd=psd, E=E):
                    e.matmul(psd[:, :], self.onesb[:], E[0], start=True, stop=False)
                    return e.matmul(psd[:, :], self.onesb[:], E[1], start=False, stop=True)
                P.op("pe", fn2, reads=[self.r_cst] + r_E, writes=[self.r_bank[bd]])
                ri = self.next_tmp()
                rt = self.tmp[ri]
                P.op("dve", lambda e, psd=psd, rt=rt: e.reciprocal(rt[:], psd[:, :]), reads=[self.r_bank[bd]], writes=[self.r_tmp[ri]])
                P.op("dve", lambda e, pso=pso, rt=rt, hd=hd, tt=tt: e.tensor_tensor(
                    self.yT[:, hd, tt * 512:(tt + 1) * 512], pso[:, :], rt[:], ALU.mult),
                    reads=[self.r_bank[bo], self.r_tmp[ri]], writes=[self.r_y[hd]])
        self.unpin(skv)

    def decay_chunk(self, bi, tt, lg_ti, k_src, k_reads, sc, phase, want_kg):
        P = self.P
        half = slice(tt * 512, (tt + 1) * 512)
        lg = self.tmp[lg_ti]
        ci = self.next_tmp()
        cum = self.tmp[ci]
        P.op("dve", lambda e: e.tensor_tensor_scan(cum[:], self.smask[:], lg[:], 0.0, ALU.mult, ALU.add),
             reads=[self.r_tmp[lg_ti], self.r_cst], writes=[self.r_tmp[ci]])
        last = cum[:].rearrange("p (c t) -> p c t", t=64)[:, :, 63]
        acol = self.Acol[bi][:, tt * 8:(tt + 1) * 8]
        P.op("act", lambda e: e.activation(acol, last, AF.Exp, scale=sc), reads=[self.r_tmp[ci]], writes=[self.r_Acol[bi]])
        P.op("act", lambda e: e.activation(lg[:], cum[:], AF.Exp, scale=-sc), reads=[self.r_tmp[ci]], writes=[self.r_tmp[lg_ti]])
        P.op("dve", lambda e: e.tensor_tensor(lg[:], k_src, lg[:], ALU.mult),
             reads=[self.r_tmp[lg_ti]] + list(k_reads), writes=[self.r_tmp[lg_ti]])
        if want_kg:
            P.op("act", lambda e: e.activation(self.kg[bi][:, half], lg[:], AF.Copy),
                 reads=[self.r_tmp[lg_ti]], writes=[self.r_kg[bi]])
        P.op("dve", lambda e: e.tensor_tensor(
            self.kdT[:, half].rearrange("p (c t) -> p c t", t=64), lg[:].rearrange("p (c t) -> p c t", t=64),
            acol.unsqueeze(2).to_broadcast([128, 8, 64]), ALU.mult),
            reads=[self.r_tmp[lg_ti], self.r_Acol[bi]], writes=[self.r_kdT])
        if phase == 1:
            P.op("act", lambda e: e.activation(cum[:], cum[:], AF.Exp, scale=sc), reads=[self.r_tmp[ci]], writes=[self.r_tmp[ci]])
            return ci
        return None

    def kd_transposes(self, bi):
        P = self.P
        for g4 in range(2):
            b = self.next_sm()
            psb = self.bank[b].bitcast(BF16)

            def fn(e, psb=psb, g4=g4):
                ins = None
                for jj in range(4):
                    jt = g4 * 4 + jj
                    ins = e.transpose(psb[:, jj * 128:(jj + 1) * 128], self.kdT[:, jt * 128:(jt + 1) * 128], self.identb[:])
                return ins
            P.op("pe", fn, reads=[self.r_kdT, self.r_cst], writes=[self.r_bank[b]])
            P.op("act", lambda e, psb=psb, g4=g4: e.activation(
                self.kdtok[bi][:, g4 * 4:(g4 + 1) * 4, :], psb[:, 0:512].rearrange("p (a b) -> p a b", b=128), AF.Copy),
                reads=[self.r_bank[b]], writes=[self.r_kdtok[bi]])

    def v_tokmajor(self, slots, dv):
        P = self.P
        nper = 512 // dv if dv == 128 else 1
        jt = 0
        while jt < NTL:
            b = self.next_acc()
            ps = self.bank[b]
            n = min(nper, NTL - jt)

            def fn(e, ps=ps, jt=jt, n=n):
                ins = None
                for jj in range(n):
                    for si_, s in enumerate(slots):
                        for k in range(KC):
                            ins = e.matmul(ps[:, jj * dv + si_ * 128: jj * dv + (si_ + 1) * 128],
                                           self.hn[:, k, (jt + jj) * 128:(jt + jj + 1) * 128], self.ring[s][:, k, :],
                                           start=(k == 0), stop=(k == KC - 1))
                return ins
            P.op("pe", fn, reads=[self.r_hn] + [self.r_ring[s] for s in slots], writes=[self.r_bank[b]])
            P.op("act", lambda e, ps=ps, jt=jt, n=n: e.activation(
                self.vtok[:, jt:jt + n, 0:dv], ps[:, 0:n * dv].rearrange("p (a b) -> p a b", b=dv), AF.Copy),
                reads=[self.r_bank[b]], writes=[self.r_vtok])
            jt += n

    def s_init(self, cout, blk_ids, dv):
        P = self.P
        W = dv + 1
        sm = self.small
        for bi, blk in enumerate(blk_ids):
            S = self.S[bi]
            P.op("dve", lambda e, S=S: e.memset(S[:, 0:W], 0.0), writes=[self.r_S[bi]])
            for r in range(NCORES - 1):
                gi = self.next_tmp()
                g = self.tmp[gi]
                P.dma("sp", g[:, 0:W], cout[r * 128:(r + 1) * 128, blk * W:(blk + 1) * W],
                      reads=[self.r_cout], writes=[self.r_tmp[gi]])
                cf = sm[:, 50 + (r % 2):51 + (r % 2)]
                P.op("dve", lambda e, g=g, cf=cf, r=r: e.tensor_scalar(cf, g[:, dv:dv + 1], self.cm[:, r:r + 1], self.cm[:, 8 + r:9 + r], ALU.mult, ALU.add),
                     reads=[self.r_tmp[gi], self.r_par], writes=[self.r_small])
                P.op("dve", lambda e, S=S, cf=cf: e.tensor_scalar(S[:, 0:dv], S[:, 0:dv], cf, None, ALU.mult),
                     reads=[self.r_small, self.r_S[bi]], writes=[self.r_S[bi]])
                P.op("dve", lambda e, S=S, g=g, r=r: e.scalar_tensor_tensor(S[:, 0:dv], g[:, 0:dv], self.cm[:, r:r + 1], S[:, 0:dv], ALU.mult, ALU.add),
                     reads=[self.r_tmp[gi], self.r_S[bi], self.r_par], writes=[self.r_S[bi]])

    def chain_and_readout(self, head, phase, gain_fn, yslot_fn):
        P = self.P
        blocks, dv, nj = head["blocks"], head["dv"], head["nj"]
        nb = len(blocks)
        if phase == 0:
            for bi in range(nb):
                P.op("dve", lambda e, bi=bi: e.memset(self.S[bi][:, 0:dv], 0.0), writes=[self.r_S[bi]])
        for jt in range(NTL):
            tt, j4 = jt // 4, jt % 4
            sb = [[None, None] for _ in range(nb)]
            for bi in range(nb):
                for hc in range(2):
                    c = jt * 2 + hc
                    rows = slice(hc * 64, hc * 64 + 64)
                    if phase == 1:
                        si_ = self.sbf_i[bi] % 3
                        self.sbf_i[bi] += 1
                        sb[bi][hc] = si_
                        P.op("act", lambda e, bi=bi, si_=si_: e.activation(self.Sbf[bi][si_][:, 0:dv], self.S[bi][:, 0:dv], AF.Copy),
                             reads=[self.r_S[bi]], writes=[self.r_Sbf[bi][si_]])
                    b = self.next_sm()
                    ps = self.bank[b]
                    P.op("pe", lambda e, ps=ps, bi=bi, rows=rows, jt=jt: e.matmul(
                        ps[:, 0:dv], self.kdtok[bi][rows, jt, :], self.vtok[rows, jt, 0:dv], start=True, stop=True),
                        reads=[self.r_kdtok[bi], self.r_vtok], writes=[self.r_bank[b]])
                    P.op("dve", lambda e, ps=ps, bi=bi, c=c: e.scalar_tensor_tensor(
                        self.S[bi][:, 0:dv], self.S[bi][:, 0:dv], self.Acol[bi][:, c:c + 1], ps[:, 0:dv], ALU.mult, ALU.add),
                        reads=[self.r_bank[b], self.r_Acol[bi], self.r_S[bi]], writes=[self.r_S[bi]])
            if phase == 0:
                continue
            b = self.next_sm()
            ps = self.bank[b]

            def fsc(e, ps=ps, jt=jt):
                ins = None
                for bi, (ch, r0, r1) in enumerate(blocks):
                    ins = e.matmul(ps[:, 0:128], self.kg[bi][r0:r1, jt * 128:(jt + 1) * 128], self.qg[bi][r0:r1, jt * 128:(jt + 1) * 128],
                                   start=(bi == 0), stop=(bi == nb - 1))
                return ins
            P.op("pe", fsc, reads=self.r_kg[0:nb] + self.r_qg[0:nb], writes=[self.r_bank[b]])
            pi = self.pm_i % 2
            self.pm_i += 1
            pm = self.Pm[pi]
            P.op("dve", lambda e, ps=ps, pm=pm: e.tensor_tensor(pm[:], ps[:, 0:128], self.maskb[:], ALU.mult),
                 reads=[self.r_bank[b], self.r_cst], writes=[self.r_Pm[pi]])
            for jc in range(nj):
                ob = 4 + jc
                po = self.bank[ob]

                def fo(e, po=po, jc=jc, jt=jt, j4=j4, pm=pm, sb=sb):
                    c0 = j4 * 128
                    ins = e.matmul(po[:, c0:c0 + 128], self.vtok[:, jt, jc * 128:(jc + 1) * 128], pm[:], start=True, stop=False)
                    for hc in range(2):
                        for bi, (ch, r0, r1) in enumerate(blocks):
                            last = (hc == 1 and bi == nb - 1)
                            ins = e.matmul(po[:, c0 + hc * 64:c0 + hc * 64 + 64],
                                           self.Sbf[bi][sb[bi][hc]][r0:r1, jc * 128:(jc + 1) * 128],
                                           self.qg[bi][r0:r1, jt * 128 + hc * 64: jt * 128 + hc * 64 + 64],
                                           start=False, stop=last)
                    return ins
                rd = [self.r_vtok, self.r_Pm[pi]] + self.r_qg[0:nb]
                for bi in range(nb):
                    rd += [self.r_Sbf[bi][sb[bi][0]], self.r_Sbf[bi][sb[bi][1]]]
                P.op("pe", fo, reads=rd, writes=[self.r_bank[ob]])
            if j4 == 3:
                self.head_norm_gate(head, tt, gain_fn, yslot_fn)

    def head_norm_gate(self, head, tt, gain_fn, yslot_fn):
        P = self.P
        dv, nj = head["dv"], head["nj"]
        pss = self.bank[7]
        for jc in range(nj):
            ti = self.next_tmp()
            sq = self.tmp[ti][:].bitcast(BF16)[:, 0:512]
            P.op("act", lambda e, sq=sq, jc=jc: e.activation(sq, self.bank[4 + jc][:, :], AF.Square),
                 reads=[self.r_bank[4 + jc]], writes=[self.r_tmp[ti]])
            P.op("pe", lambda e, sq=sq, jc=jc: e.matmul(pss[:, :], self.onesb[:], sq, start=(jc == 0), stop=(jc == nj - 1)),
                 reads=[self.r_tmp[ti], self.r_cst], writes=[self.r_bank[7]])
        P.op("act", lambda e: e.activation(self.rstd[:], pss[:, :], AF.Sqrt, scale=1.0 / dv, bias=EPS),
             reads=[self.r_bank[7]], writes=[self.r_rstd])
        P.op("dve", lambda e: e.reciprocal(self.rstd[:], self.rstd[:]), reads=[self.r_rstd], writes=[self.r_rstd])
        for jc in range(nj):
            ti = self.next_tmp()
            tm = self.tmp[ti]
            P.op("dve", lambda e, tm=tm, jc=jc: e.scalar_tensor_tensor(
                tm[:], self.bank[4 + jc][:, :], gain_fn(jc), self.rstd[:], ALU.mult, ALU.mult),
                reads=[self.r_bank[4 + jc], self.r_rstd, self.r_par], writes=[self.r_tmp[ti]])
            ys = yslot_fn(jc)
            P.op("dve", lambda e, tm=tm, jc=jc, ys=ys: e.tensor_tensor(
                self.yT[:, ys, tt * 512:(tt + 1) * 512], tm[:], self.gate[:, jc, tt * 512:(tt + 1) * 512], ALU.mult),
                reads=[self.r_tmp[ti], self.r_gate], writes=[self.r_y[ys]])

    def send_state(self, head, cin, blk_ids, a_from):
        P = self.P
        dv = head["dv"]
        W = dv + 1
        for bi, blk in enumerate(blk_ids):
            P.op("dve", lambda e, bi=bi: e.tensor_reduce(self.S[bi][:, dv:dv + 1], self.Acol[bi][:, 0:16], AX.X, ALU.mult),
                 reads=[self.r_Acol[bi], self.r_S[bi]], writes=[self.r_S[bi]])
            P.dma("sp", cin[:, blk * W:(blk + 1) * W], self.S[bi][:, 0:W], reads=[self.r_S[bi]], writes=[self.r_cin])

    def hgrn_head(self, l, hd, phase, cin, cout):
        P = self.P
        j = l // 2
        win = self.w_in[l]
        head = hgrn_heads()[hd]
        bi = 0
        sf = self.load_w(win[:, 1536 + hd * 128:1536 + (hd + 1) * 128])
        sq = self.load_w(win[:, hd * 128:(hd + 1) * 128]) if phase == 1 else None
        lb = self.lbt[:, 2 * j, hd:hd + 1]
        oml = self.lbt[:, 2 * j + 1, hd:hd + 1]
        if phase == 1:
            self.s_init(cout, [hd], 128)
        for tt in range(NT):
            half = slice(tt * 512, (tt + 1) * 512)
            b = self.gemm(sf, 128, self.hn_rhs, KC, tt, [self.r_hn])
            fi = self.next_tmp()
            li = self.next_tmp()
            f, lg = self.tmp[fi], self.tmp[li]
            P.op("act", lambda e, b=b, f=f: e.activation(f[:], self.bank[b][:, :], AF.Sigmoid), reads=[self.r_bank[b]], writes=[self.r_tmp[fi]])
            P.op("dve", lambda e, f=f: e.tensor_scalar(f[:], f[:], oml, lb, ALU.mult, ALU.add), reads=[self.r_tmp[fi], self.r_cst], writes=[self.r_tmp[fi]])
            P.op("act", lambda e, f=f, lg=lg: e.activation(lg[:], f[:], AF.Ln), reads=[self.r_tmp[fi]], writes=[self.r_tmp[li]])
            P.op("dve", lambda e, f=f: e.tensor_scalar(f[:], f[:], -1.0, 1.0, ALU.mult, ALU.add), reads=[self.r_tmp[fi]], writes=[self.r_tmp[fi]])
            ei = self.decay_chunk(bi, tt, li, f[:], [self.r_tmp[fi]], 1.0, phase, want_kg=(phase == 1))
            if phase == 1:
                b = self.gemm(sq, 128, self.hn_rhs, KC, tt, [self.r_hn])
                qi = self.next_tmp()
                q = self.tmp[qi]
                P.op("act", lambda e, b=b, q=q: e.activation(q[:], self.bank[b][:, :], AF.Silu), reads=[self.r_bank[b]], writes=[self.r_tmp[qi]])
                P.op("dve", lambda e, q=q, ei=ei, half=half: e.tensor_tensor(self.qg[bi][:, half], q[:], self.tmp[ei][:], ALU.mult),
                     reads=[self.r_tmp[qi], self.r_tmp[ei]], writes=[self.r_qg[bi]])
        si_ = self.load_w(win[:, 3072 + hd * 128:3072 + (hd + 1) * 128])
        self.v_tokmajor([si_], 128)
        if phase == 1:
            sg = self.load_w(win[:, 4608 + hd * 128:4608 + (hd + 1) * 128])
            for tt in range(NT):
                b = self.gemm(sg, 128, self.hn_rhs, KC, tt, [self.r_hn])
                P.op("act", lambda e, b=b, tt=tt: e.activation(self.gate[:, 0, tt * 512:(tt + 1) * 512], self.bank[b][:, :], AF.Silu),
                     reads=[self.r_bank[b]], writes=[self.r_gate])
        self.kd_transposes(bi)
        ych = hd
        self.chain_and_readout(head, phase, lambda jc: self.pcol("onh", j), lambda jc: self.yslot(ych + jc))
        if phase == 0:
            self.send_state(head, cin, [hd], None)

    def gla_gklow(self, l):
        P = self.P
        P.dma("pool", self.wgk_sb[:], self.wgk[l // 2], reads=[self.r_gklow], writes=[self.r_gklow])
        s = self.load_w(self.w_in[l][:, 4608:4624], ncols=16)
        for tt in range(NT):
            b = self.next_acc()
            ps = self.bank[b]

            def fn(e, ps=ps, s=s, tt=tt):
                ins = None
                for k in range(KC):
                    ins = e.matmul(ps[0:16, :], self.ring[s][:, k, 0:16], self.hn_rhs(k, tt), start=(k == 0), stop=(k == KC - 1))
                return ins
            P.op("pe", fn, reads=[self.r_ring[s], self.r_hn], writes=[self.r_bank[b]])
            P.op("act", lambda e, ps=ps, tt=tt: e.activation(self.gklow[:, tt * 512:(tt + 1) * 512], ps[0:16, :], AF.Copy),
                 reads=[self.r_bank[b]], writes=[self.r_gklow])

    def gla_head(self, l, hd, phase, cin, cout):
        P = self.P
        j = l // 2
        win = self.w_in[l]
        head = gla_heads()[hd]
        blocks = head["blocks"]
        blk_ids = [hd * 2, hd * 2 + 1]
        sc = -1.0 / 16.0
        if phase == 1:
            self.s_init(cout, blk_ids, 384)
        for bi, (ch, r0, r1) in enumerate(blocks):
            sk = self.load_w(win[:, 768 + ch * 128:768 + (ch + 1) * 128])
            sq = self.load_w(win[:, ch * 128:(ch + 1) * 128]) if phase == 1 else None
            for tt in range(NT):
                half = slice(tt * 512, (tt + 1) * 512)
                bz = self.next_acc()
                pz = self.bank[bz]
                P.op("pe", lambda e, pz=pz, ch=ch, tt=tt: e.matmul(
                    pz[:, :], self.wgk_sb[0:16, ch * 128:(ch + 1) * 128], self.gklow[0:16, tt * 512:(tt + 1) * 512], start=True, stop=True),
                    reads=[self.r_gklow], writes=[self.r_bank[bz]])
                li = self.next_tmp()
                lg = self.tmp[li]
                nb_ = self.nbgk[:, j * 6 + ch: j * 6 + ch + 1]
                P.op("act", lambda e, pz=pz, lg=lg, nb_=nb_: e.activation(lg[:], pz[:, :], AF.Exp, bias=nb_, scale=-1.0),
                     reads=[self.r_bank[bz], self.r_cst], writes=[self.r_tmp[li]])
                P.op("act", lambda e, lg=lg: e.activation(lg[:], lg[:], AF.Ln, bias=1.0), reads=[self.r_tmp[li]], writes=[self.r_tmp[li]])
                bk = self.gemm(sk, 128, self.hn_rhs, KC, tt, [self.r_hn])
                ei = self.decay_chunk(bi, tt, li, self.bank[bk][:, :], [self.r_bank[bk]], sc, phase, want_kg=(phase == 1))
                if phase == 1:
                    bq = self.gemm(sq, 128, self.hn_rhs, KC, tt, [self.r_hn])
                    P.op("dve", lambda e, bq=bq, ei=ei, half=half, bi=bi: e.scalar_tensor_tensor(
                        self.qg[bi][:, half], self.bank[bq][:, :], 192 ** -0.5, self.tmp[ei][:], ALU.mult, ALU.mult),
                        reads=[self.r_bank[bq], self.r_tmp[ei]], writes=[self.r_qg[bi]])
            self.kd_transposes(bi)
        vs = [self.load_w(win[:, 1536 + hd * 384 + jc * 128:1536 + hd * 384 + (jc + 1) * 128]) for jc in range(3)]
        self.v_tokmajor(vs, 384)
        if phase == 1:
            for jc in range(3):
                sg = self.load_w(win[:, 3072 + hd * 384 + jc * 128:3072 + hd * 384 + (jc + 1) * 128])
                for tt in range(NT):
                    b = self.gemm(sg, 128, self.hn_rhs, KC, tt, [self.r_hn])
                    P.op("act", lambda e, b=b, tt=tt, jc=jc: e.activation(self.gate[:, jc, tt * 512:(tt + 1) * 512], self.bank[b][:, :], AF.Silu),
                         reads=[self.r_bank[b]], writes=[self.r_gate])
        ych = 3 * hd
        self.chain_and_readout(head, phase, lambda jc: self.pcol("ong", j, jc), lambda jc: self.yslot(ych + jc))
        if phase == 0:
            self.send_state(head, cin, blk_ids, None)

    def yslot(self, ychunk):
        return 4 + ychunk if ychunk < 6 else ychunk - 6

    def out_proj_part(self, l, part):
        P = self.P
        wo = self.w_out[l]
        if part == 0:
            r0, nk = 0, 10
            slots = list(range(10))
        else:
            r0, nk = 1280, 6
            slots = list(range(6))
        for f in range(KC):
            s = self.load_w(wo[r0:r0 + nk * 128, f * 128:(f + 1) * 128], nk=nk)
            for tt in range(NT):
                b = self.gemm(s, 128, lambda k, tt_: self.yT[:, slots[k], tt_ * 512:(tt_ + 1) * 512], nk, tt,
                              [self.r_y[i] for i in slots])
                P.op("dve", lambda e, b=b, f=f, tt=tt: e.tensor_tensor(self.hs(f, tt), self.bank[b][:, :], self.hs(f, tt), ALU.add),
                     reads=[self.r_bank[b], self.r_h[f][tt]], writes=[self.r_h[f][tt]])

    def ffn(self, l):
        P = self.P
        wgu, wdn = self.w_gu[l], self.w_dn[l]
        G = 2
        for g in range(NFF // G):
            aslots = [(g % 2) * G + i for i in range(G)]
            dsl = []
            for i in range(G):
                jj = g * G + i
                sg = self.load_w(wgu[:, jj * 128:(jj + 1) * 128])
                su = self.load_w(wgu[:, DFF + jj * 128:DFF + (jj + 1) * 128])
                for tt in range(NT):
                    bg = self.gemm(sg, 128, self.hn_rhs, KC, tt, [self.r_hn])
                    ti = self.next_tmp()
                    tm = self.tmp[ti]
                    P.op("act", lambda e, bg=bg, tm=tm: e.activation(tm[:], self.bank[bg][:, :], AF.Silu),
                         reads=[self.r_bank[bg]], writes=[self.r_tmp[ti]])
                    bu = self.gemm(su, 128, self.hn_rhs, KC, tt, [self.r_hn])
                    P.op("dve", lambda e, bu=bu, tm=tm, a=aslots[i], tt=tt: e.tensor_tensor(
                        self.yT[:, a, tt * 512:(tt + 1) * 512], self.bank[bu][:, :], tm[:], ALU.mult),
                        reads=[self.r_bank[bu], self.r_tmp[ti]], writes=[self.r_y[aslots[i]]])
                dsl.append(self.load_w_rows(wdn[jj * 128:(jj + 1) * 128, :]))
            for f in range(KC):
                for tt in range(NT):
                    b = self.next_acc()
                    ps = self.bank[b]

                    def fn(e, ps=ps, f=f, tt=tt, dsl=tuple(dsl), aslots=tuple(aslots)):
                        ins = None
                        for i in range(G):
                            ins = e.matmul(ps[:, :], self.ring[dsl[i]][:, f, :], self.yT[:, aslots[i], tt * 512:(tt + 1) * 512],
                                           start=(i == 0), stop=(i == G - 1))
                        return ins
                    P.op("pe", fn, reads=[self.r_ring[s] for s in dsl] + [self.r_y[a] for a in aslots], writes=[self.r_bank[b]])
                    P.op("dve", lambda e, b=b, f=f, tt=tt: e.tensor_tensor(self.hs(f, tt), self.bank[b][:, :], self.hs(f, tt), ALU.add),
                         reads=[self.r_bank[b], self.r_h[f][tt]], writes=[self.r_h[f][tt]])

    def layer(self, si, l, step_idx):
        P, nc = self.P, self.nc
        hgrn = (l % 2 == 0)
        nblk, dv = (12, 128) if hgrn else (8, 384)
        W = dv + 1
        cin = nc.dram_tensor("cin_%d_%d" % (step_idx, l), [128, nblk * W], F32)
        cout = nc.dram_tensor("cout_%d_%d" % (step_idx, l), [NCORES * 128, nblk * W], F32)
        self.r_cin, self.r_cout = P.res(), P.res()
        self.acc_set, self.sm_set = [0, 1], [2, 3]
        self.norm_to_hn("nmix", l)
        nheads = 12 if hgrn else 4
        headfn = self.hgrn_head if hgrn else self.gla_head
        if not hgrn:
            self.gla_gklow(l)
        for hd in range(nheads):
            headfn(l, hd, 0, cin, cout)
        P.op("pool", lambda e: e.collective_compute("AllGather", ALU.bypass, replica_groups=[list(range(NCORES))],
                                                    ins=[cin.ap().opt()], outs=[cout.ap().opt()]),
             reads=[self.r_cin], writes=[self.r_cout])
        self.xattn(si, l)
        for hd in range(nheads):
            headfn(l, hd, 1, cin, cout)
            if (hgrn and hd == 5) or ((not hgrn) and hd == 1):
                self.acc_set = [0, 1]
                self.out_proj_part(l, 0)
        self.out_proj_part(l, 1)
        self.acc_set, self.sm_set = [0, 1, 2, 3, 4, 5, 6, 7], [2, 3]
        self.norm_to_hn("nffn", l)
        self.ffn(l)
        self.acc_set = [0, 1]

    def build(self):
        self.prologue()
        for idx, st in enumerate(self.steps):
            si = idx
            if st["in_fm"]:
                self.load_fm(si)
            else:
                self.acc_set = [0, 1, 2, 3, 4, 5, 6, 7]
                self.load_x(si)
                self.acc_set = [0, 1]
            for l in st["layers"]:
                self.layer(si, l, idx)
            if st["final"]:
                self.acc_set = [0, 1, 2, 3, 4, 5, 6, 7]
                self.final_out(si)
                self.acc_set = [0, 1]
            else:
                self.store_fm(si)
        self.P.finish()
        self.P.emit()
        return self.nc


def _consts():
    c = np.zeros((128, 384), np.float32)
    s = np.arange(128)[:, None]
    t = np.arange(128)[None, :]
    c[:, 0:128] = ((s // 64 == t // 64) & (s <= t)).astype(np.float32)
    c[:, 128:256] = np.eye(128, dtype=np.float32)
    c[:, 256:384] = np.eye(128, dtype=np.float32)
    return c


def _params(inp):
    p = np.zeros((128, Builder.NPAR), np.float32)
    f32 = lambda a: np.asarray(a, np.float32)

    def fm(v, nchunk):
        return f32(v).reshape(nchunk, 128).T
    for l in range(DEPTH):
        p[:, 0 + l * 16: 16 + l * 16] = fm(inp["norm_mix"][l], 16)
        p[:, 64 + l * 16: 80 + l * 16] = fm(inp["norm_ffn"][l], 16)
    p[:, 128:144] = fm(inp["norm_mem"], 16)
    p[:, 144:160] = fm(inp["norm_final"], 16)
    for j in range(2):
        p[:, 160 + j * 12:172 + j * 12] = fm(inp["hgrn_lb_logits"][j], 12)
        p[:, 184 + j] = f32(inp["hgrn_onorm"][j])
        p[:, 186 + j * 3:189 + j * 3] = fm(inp["gla_onorm"][j], 3)
        p[:, 192 + j * 6:198 + j * 6] = fm(inp["gla_b_gk"][j], 6)
    return p


def _cmask(c):
    m = np.zeros((128, 16), np.float32)
    for r in range(8):
        m[:, r] = 1.0 if r < c else 0.0
        m[:, 8 + r] = 1.0 - m[:, r]
    return m


_CACHE = {}


def _program(steps_key, steps):
    if steps_key not in _CACHE:
        _CACHE[steps_key] = Builder(steps).build()
    return _CACHE[steps_key]


PLAN = "fused"


def _weights_for(inp, layers):
    w = {}
    f32 = lambda a: np.ascontiguousarray(np.asarray(a, np.float32))
    for l in layers:
        j = l // 2
        w["w_in%d" % l] = f32(inp["hgrn_w_in"][j] if l % 2 == 0 else inp["gla_w_in"][j])
        w["w_kv%d" % l] = f32(inp["w_mem_kv"][l])
        wo = np.asarray(inp["w_out"][l], np.float32)
        w["w_out%d" % l] = np.ascontiguousarray(np.concatenate([wo[1536:2048], wo[0:1536]], axis=0))
        w["w_gu%d" % l] = f32(inp["w_gate_up"][l])
        w["w_dn%d" % l] = f32(inp["w_down"][l])
    return w


def kernel(**inp):
    x = np.asarray(inp["x"], np.float32)
    mem = np.asarray(inp["mem"], np.float32)
    consts = _consts()
    params = _params(inp)
    wgk = np.ascontiguousarray(np.asarray(inp["gla_w_gk"], np.float32))
    out = np.zeros((2, 8192, D), np.float32)
    common = {"consts": consts, "params": params, "wgk": wgk}

    def run(steps, sweeps, feeds):
        key = repr(steps)
        nc = _program(key, steps)
        layers = sorted({l for st in steps for l in st["layers"]})
        w = _weights_for(inp, layers)
        in_maps = []
        for c in range(NCORES):
            m = dict(common)
            m.update(w)
            m["cmask"] = _cmask(c)
            m["mem"] = np.ascontiguousarray(np.stack([mem[s] for s in sweeps]))
            m.update(feeds(c))
            in_maps.append(m)
        return run_bass_kernel_spmd(nc, in_maps, core_ids=list(range(NCORES))).results

    def xfeed(sweeps):
        return lambda c: {"hin": np.ascontiguousarray(np.stack([x[s, c * T:(c + 1) * T] for s in sweeps]))}

    if PLAN == "fused":
        steps = [dict(layers=[0, 1, 2, 3], in_fm=False, final=True) for _ in range(2)]
        res = run(steps, [0, 1], xfeed([0, 1]))
        for c in range(NCORES):
            for s in range(2):
                out[s, c * T:(c + 1) * T] = res[c]["hout"][s]
    elif PLAN == "per_sweep":
        for s in range(2):
            steps = [dict(layers=[0, 1, 2, 3], in_fm=False, final=True)]
            res = run(steps, [s], xfeed([s]))
            for c in range(NCORES):
                out[s, c * T:(c + 1) * T] = res[c]["hout"][0]
    else:
        for s in range(2):
            hfm = None
            for l in range(DEPTH):
                steps = [dict(layers=[l], in_fm=(l > 0), final=(l == DEPTH - 1))]
                if l == 0:
                    feeds = xfeed([s])
                else:
                    feeds = (lambda hf: (lambda c: {"hfm_in": hf[c]}))(hfm)
                res = run(steps, [s], feeds)
                if l == DEPTH - 1:
                    for c in range(NCORES):
                        out[s, c * T:(c + 1) * T] = res[c]["hout"][0]
                else:
                    hfm = [res[c]["hfm_out"] for c in range(NCORES)]
    return out
```
